# Optimizing a Trainium2 kernel written in Bass

```python
import jax, jax.numpy as jnp
from jax import lax
import numpy as np

D_MODEL = 4096
BATCH = 2
SEQ = 8192
DEPTH = 2

GRID_W = 64
CTX_LEN = 256
F_DIM = 1024
N_HEADS = 16
Q_LORA = 768
KV_LORA = 512
QK_NOPE = 128
QK_ROPE = 64
V_DIM = 128
QK_DIM = QK_NOPE + QK_ROPE
MLA_DIM = N_HEADS * V_DIM
ROPE_BASE = 10000.0
Q_BLOCK = 128
C_DIM = 1024
CONV_W = 31
N_BRANCH = 3
EPS = 1e-6

OFF_F = 0
OFF_F_GATE = OFF_F + F_DIM
OFF_Q = OFF_F_GATE + F_DIM
OFF_KV = OFF_Q + Q_LORA
OFF_KROPE = OFF_KV + KV_LORA
OFF_M_GATE = OFF_KROPE + QK_ROPE
OFF_GLU = OFF_M_GATE + MLA_DIM
OFF_C_GATE = OFF_GLU + 2 * C_DIM
OFF_MERGE = OFF_C_GATE + C_DIM
N_IN = OFF_MERGE + N_BRANCH * D_MODEL

kernel_name = 'hybrid_fourier_mla_conformer_dit_block'


def rms_norm(x, g):
    xf = x.astype(jnp.float32)
    y = xf * lax.rsqrt(jnp.mean(xf * xf, axis=-1, keepdims=True) + EPS)
    return (y * g.astype(jnp.float32)).astype(x.dtype)


def layer_norm(x, g, b):
    xf = x.astype(jnp.float32)
    mu = jnp.mean(xf, axis=-1, keepdims=True)
    d = xf - mu
    var = jnp.mean(d * d, axis=-1, keepdims=True)
    return (d * lax.rsqrt(var + EPS) * g.astype(jnp.float32) + b.astype(jnp.float32)).astype(x.dtype)


def rope_half(x, pos):
    half = x.shape[-1] // 2
    inv = ROPE_BASE ** (-jnp.arange(half, dtype=jnp.float32) / half)
    ang = pos.astype(jnp.float32)[:, None] * inv[None, :]
    cos = jnp.cos(ang)[:, None, :].astype(x.dtype)
    sin = jnp.sin(ang)[:, None, :].astype(x.dtype)
    x1, x2 = x[..., :half], x[..., half:]
    return jnp.concatenate([x1 * cos - x2 * sin, x1 * sin + x2 * cos], axis=-1)


def axial_rope(x, rows, cols):
    h = x.shape[-1] // 2
    return jnp.concatenate([rope_half(x[..., :h], rows), rope_half(x[..., h:], cols)], axis=-1)


def mla_queries(cq, q_a_norm, w_uq, q_norm, rows, cols):
    b, s, _ = cq.shape
    q = (rms_norm(cq, q_a_norm) @ w_uq).reshape(b, s, N_HEADS, QK_DIM)
    q = rms_norm(q, q_norm)
    if rows is None:
        return q
    return jnp.concatenate([q[..., :QK_NOPE], axial_rope(q[..., QK_NOPE:], rows, cols)], axis=-1)


def mla_keys_values(ckv, k_rope, kv_a_norm, w_ukv, k_norm, rows, cols):
    b, s, _ = ckv.shape
    kv = (rms_norm(ckv, kv_a_norm) @ w_ukv).reshape(b, s, N_HEADS, QK_NOPE + V_DIM)
    k_nope, v = kv[..., :QK_NOPE], kv[..., QK_NOPE:]
    k_r = jnp.broadcast_to(k_rope[:, :, None, :], (b, s, N_HEADS, QK_ROPE))
    k = rms_norm(jnp.concatenate([k_nope, k_r], axis=-1), k_norm)
    if rows is not None:
        k = jnp.concatenate([k[..., :QK_NOPE], axial_rope(k[..., QK_NOPE:], rows, cols)], axis=-1)
    return k, v


def attend(q, k, v):
    s = jnp.einsum('bqhd,bkhd->bhqk', q, k).astype(jnp.float32) * (QK_DIM ** -0.5)
    p = jax.nn.softmax(s, axis=-1).astype(v.dtype)
    return jnp.einsum('bhqk,bkhd->bqhd', p, v)


def blocked_attend(q, k, v):
    b, s, h, dh = q.shape
    nb = s // Q_BLOCK
    qb = q.reshape(b, nb, Q_BLOCK, h, dh).transpose(1, 0, 2, 3, 4)
    ob = lax.map(lambda blk: attend(blk, k, v), qb)
    return ob.transpose(1, 0, 2, 3, 4).reshape(b, s, h * V_DIM)


def fourier_mix(u, w_fnet):
    z = jnp.fft.fft2(u.astype(jnp.float32), axes=(1, 2), norm='ortho').real.astype(u.dtype)
    return z @ w_fnet


def conformer_conv(glu_in, conv_w, conv_b, cln_g, cln_b, w_pw2):
    a, g = jnp.split(glu_in, 2, axis=-1)
    u = a * jax.nn.sigmoid(g)
    u = lax.conv_general_dilated(u, conv_w[:, None, :], window_strides=(1,),
                                 padding=((CONV_W // 2, CONV_W // 2),),
                                 dimension_numbers=('NWC', 'WIO', 'NWC'),
                                 feature_group_count=C_DIM) + conv_b
    u = jax.nn.silu(layer_norm(u, cln_g, cln_b))
    return u @ w_pw2


def mix_and_merge(z, attn, w_fnet, conv_w, conv_b, cln_g, cln_b, w_pw2, w_br_f, w_br_m, w_br_c, w_out):
    y_f = fourier_mix(z[..., OFF_F:OFF_F_GATE], w_fnet) * jax.nn.silu(z[..., OFF_F_GATE:OFF_Q])
    y_m = attn * jax.nn.silu(z[..., OFF_M_GATE:OFF_GLU])
    y_c = conformer_conv(z[..., OFF_GLU:OFF_C_GATE], conv_w, conv_b, cln_g, cln_b, w_pw2) \
        * jax.nn.silu(z[..., OFF_C_GATE:OFF_MERGE])
    g_f, g_m, g_c = jnp.split(jax.nn.sigmoid(z[..., OFF_MERGE:]), N_BRANCH, axis=-1)
    merged = g_f * (y_f @ w_br_f) + g_m * (y_m @ w_br_m) + g_c * (y_c @ w_br_c)
    return merged @ w_out


def setup_inputs(seed: int = 0) -> dict:
    key = jax.random.key(seed)
    ks = jax.random.split(key, 24)
    L, D = DEPTH, D_MODEL
    nrm = lambda k, shape, scale: jax.random.normal(k, shape, dtype=jnp.float32) * scale
    gain = lambda k, shape: 1.0 + 0.05 * jax.random.normal(k, shape, dtype=jnp.float32)
    return {
        'x': nrm(ks[0], (BATCH, SEQ, D), 1.0),
        'c': nrm(ks[1], (BATCH, D), 1.0),
        'ctx': nrm(ks[2], (BATCH, CTX_LEN, D), 1.0),
        'c_ctx': nrm(ks[3], (D,), 1.0),
        'norm_g': gain(ks[4], (L, D)),
        'w_ada': nrm(ks[5], (L, D, 3 * D), 0.5 * D ** -0.5),
        'b_ada': nrm(ks[6], (L, 3 * D), 0.02),
        'w_in': nrm(ks[7], (L, D, N_IN), D ** -0.5),
        'q_a_norm': gain(ks[8], (L, Q_LORA)),
        'w_uq': nrm(ks[9], (L, Q_LORA, N_HEADS * QK_DIM), Q_LORA ** -0.5),
        'kv_a_norm': gain(ks[10], (L, KV_LORA)),
        'w_ukv': nrm(ks[11], (L, KV_LORA, N_HEADS * (QK_NOPE + V_DIM)), KV_LORA ** -0.5),
        'q_norm': gain(ks[12], (L, QK_DIM)),
        'k_norm': gain(ks[13], (L, QK_DIM)),
        'w_fnet': nrm(ks[14], (L, F_DIM, F_DIM), F_DIM ** -0.5),
        'conv_w': nrm(ks[15], (L, CONV_W, C_DIM), CONV_W ** -0.5),
        'conv_b': nrm(ks[16], (L, C_DIM), 0.02),
        'cln_g': gain(ks[17], (L, C_DIM)),
        'cln_b': nrm(ks[18], (L, C_DIM), 0.02),
        'w_pw2': nrm(ks[19], (L, C_DIM, C_DIM), C_DIM ** -0.5),
        'w_br_f': nrm(ks[20], (L, F_DIM, D), F_DIM ** -0.5),
        'w_br_m': nrm(ks[21], (L, MLA_DIM, D), MLA_DIM ** -0.5),
        'w_br_c': nrm(ks[22], (L, C_DIM, D), C_DIM ** -0.5),
        'w_out': nrm(ks[23], (L, D, D), D ** -0.5),
    }


def reference(x, c, ctx, c_ctx, norm_g, w_ada, b_ada, w_in, q_a_norm, w_uq, kv_a_norm, w_ukv,
              q_norm, k_norm, w_fnet, conv_w, conv_b, cln_g, cln_b, w_pw2, w_br_f, w_br_m,
              w_br_c, w_out):
    b, n_tok, _ = x.shape
    n_ctx = ctx.shape[1]
    n_rows = n_tok // GRID_W
    rows = jnp.repeat(jnp.arange(n_rows, dtype=jnp.int32), GRID_W)
    cols = jnp.tile(jnp.arange(GRID_W, dtype=jnp.int32), n_rows)
    xl, xc = x, ctx
    for l in range(DEPTH):
        last = l == DEPTH - 1
        shift_l, scale_l, gate_l = jnp.split(jax.nn.silu(c) @ w_ada[l] + b_ada[l], 3, axis=-1)
        n_mod_c = 2 if last else 3
        mod_c = jax.nn.silu(c_ctx) @ w_ada[l][:, :n_mod_c * D_MODEL] + b_ada[l][:n_mod_c * D_MODEL]
        shift_c, scale_c = mod_c[:D_MODEL], mod_c[D_MODEL:2 * D_MODEL]
        hl = rms_norm(xl, norm_g[l]) * (1 + scale_l[:, None, :]) + shift_l[:, None, :]
        hc = rms_norm(xc, norm_g[l]) * (1 + scale_c) + shift_c
        zl = hl @ w_in[l]
        if last:
            zkv_c = hc @ w_in[l][:, OFF_KV:OFF_M_GATE]
        else:
            zc = hc @ w_in[l]
            zkv_c = zc[..., OFF_KV:OFF_M_GATE]
        kc, vc = mla_keys_values(zkv_c[..., :KV_LORA], zkv_c[..., KV_LORA:], kv_a_norm[l],
                                 w_ukv[l], k_norm[l], None, None)
        kl, vl = mla_keys_values(zl[..., OFF_KV:OFF_KROPE], zl[..., OFF_KROPE:OFF_M_GATE],
                                 kv_a_norm[l], w_ukv[l], k_norm[l], rows, cols)
        ql = mla_queries(zl[..., OFF_Q:OFF_KV], q_a_norm[l], w_uq[l], q_norm[l], rows, cols)
        ol = blocked_attend(ql, jnp.concatenate([kc, kl], axis=1), jnp.concatenate([vc, vl], axis=1))
        branch_w = (w_fnet[l], conv_w[l], conv_b[l], cln_g[l], cln_b[l], w_pw2[l],
                    w_br_f[l], w_br_m[l], w_br_c[l], w_out[l])
        new_xl = xl + gate_l[:, None, :] * mix_and_merge(zl, ol, *branch_w)
        if not last:
            qc = mla_queries(zc[..., OFF_Q:OFF_KV], q_a_norm[l], w_uq[l], q_norm[l], None, None)
            oc = attend(qc, kc, vc).reshape(b, n_ctx, MLA_DIM)
            xc = xc + mod_c[2 * D_MODEL:] * mix_and_merge(zc, oc, *branch_w)
        xl = new_xl
    return xl
```

```python
import numpy as np
import ml_dtypes
import concourse.bass as bass
import concourse.mybir as mybir

F32 = mybir.dt.float32
BF16 = mybir.dt.bfloat16
AF = mybir.ActivationFunctionType
ALU = mybir.AluOpType
AX = mybir.AxisListType

SELF_SYNC = True


class Prog:
    def __init__(self, nc):
        self.nc = nc
        self.order = ['pe', 'act', 'dve', 'pool', 'sp']
        self.ops = {e: [] for e in self.order}
        self.esem = {e: nc.alloc_semaphore("es_" + e) for e in ['pe', 'act', 'dve', 'pool']}
        self.dsem = {}
        self.lastw = {}
        self.readers = {}
        self.psn = 0
        self.fence_ev = []

    def _dsem(self, k):
        if k not in self.dsem:
            self.dsem[k] = [self.nc.alloc_semaphore("ds_%d" % len(self.dsem)), 0]
        return self.dsem[k]

    def _deps(self, eng, reads, writes):
        ev = []
        for k in reads:
            if k in self.lastw:
                ev.append(self.lastw[k])
        for k in writes:
            if k in self.lastw:
                ev.append(self.lastw[k])
            ev.extend(self.readers.get(k, []))
        ev = list(self.fence_ev) + ev
        out = []
        for e in ev:
            if e[0] == 'e' and e[1] == eng and (eng in ('pe', 'sp') or not SELF_SYNC):
                continue
            if e not in out:
                out.append(e)
        return out

    def _commit(self, me, reads, writes):
        for k in writes:
            self.lastw[k] = me
            self.readers[k] = []
        for k in reads:
            self.readers.setdefault(k, []).append(me)

    def op(self, eng, name, kw, reads=(), writes=()):
        fn = (lambda e, name=name, kw=kw: getattr(e, name)(**kw))
        waits = self._deps(eng, reads, writes)
        idx = len(self.ops[eng])
        self.ops[eng].append(dict(kind='c', fn=fn, waits=waits, sig=False))
        self._commit(('e', eng, idx), reads, writes)

    def dma(self, q, out, in_, reads=(), writes=(), semkey=None):
        if semkey is None:
            semkey = writes[0]
        waits = self._deps(q, reads, writes)
        s = self._dsem(semkey)
        s[1] += 16
        self.ops[q].append(dict(kind='d', fn=(lambda e, o=out, i=in_: e.dma_start(out=o, in_=i)),
                                waits=waits, sig=False, dsem=(s[0], 16)))
        self._commit(('d', semkey, s[1]), reads, writes)

    def fence(self):
        ev = []
        for e in ['pe', 'act', 'dve', 'pool']:
            n = len([o for o in self.ops[e]])
            for i in range(n - 1, -1, -1):
                if self.ops[e][i]['kind'] == 'c':
                    ev.append(('e', e, i))
                    break
        for k, (sem, cnt) in self.dsem.items():
            if cnt > 0:
                ev.append(('d', k, cnt))
        self.fence_ev = ev

    def finish_wait(self, q, keys):
        waits = []
        for k in keys:
            if k in self.lastw:
                waits.append(self.lastw[k])
        self.ops[q].append(dict(kind='w', fn=None, waits=waits, sig=False))

    def emit(self):
        nc = self.nc
        for e in self.order:
            for o in self.ops[e]:
                for w in o['waits']:
                    if w[0] == 'e':
                        self.ops[w[1]][w[2]]['sig'] = True
        sval = {}
        for e in ['pe', 'act', 'dve', 'pool']:
            c = 0
            for i, o in enumerate(self.ops[e]):
                if o['sig']:
                    c += 1
                    sval[(e, i)] = c
        self.sigcounts = {e: sum(1 for o in self.ops[e] if o['sig']) for e in self.ops}
        engobj = {'pe': 'tensor', 'act': 'scalar', 'dve': 'vector', 'pool': 'gpsimd', 'sp': 'sync'}
        with nc.Block() as block:
            for e in self.order:
                def body(eng, e=e):
                    seen = {}
                    for o in self.ops[e]:
                        need = {}
                        for w in o['waits']:
                            if w[0] == 'e':
                                sem, v = self.esem[w[1]], sval[(w[1], w[2])]
                            else:
                                sem, v = self.dsem[w[1]][0], w[2]
                            key = id(sem)
                            if seen.get(key, 0) >= v:
                                continue
                            if key not in need or need[key][1] < v:
                                need[key] = (sem, v)
                        for key, (sem, v) in need.items():
                            eng.wait_ge(sem, v)
                            seen[key] = v
                        if o['kind'] == 'w':
                            continue
                        ins = o['fn'](eng)
                        if o['kind'] == 'd':
                            ins.then_inc(o['dsem'][0], 16)
                        elif o['sig']:
                            ins.then_inc(self.esem[e], 1)
                getattr(block, engobj[e])(body)


EPS = 1e-6


def mkcfg(D=4096, F=1024, QL=768, KVL=512, NH=16, NOPE=128, ROPE=64, VD=128, C=1024, CW=31,
          T=2048, HB=128, NK=8448, NS=8192):
    c = dict(D=D, F=F, QL=QL, KVL=KVL, NH=NH, NOPE=NOPE, ROPE=ROPE, VD=VD, C=C, CW=CW, T=T, HB=HB, NK=NK, NS=NS)
    c['QK'] = NOPE + ROPE
    c['MLA'] = NH * VD
    c['OFF_F'] = 0
    c['OFF_FG'] = F
    c['OFF_Q'] = 2 * F
    c['OFF_KV'] = c['OFF_Q'] + QL
    c['OFF_KR'] = c['OFF_KV'] + KVL
    c['OFF_MG'] = c['OFF_KR'] + ROPE
    c['OFF_GLU'] = c['OFF_MG'] + c['MLA']
    c['OFF_CG'] = c['OFF_GLU'] + 2 * C
    c['OFF_MERGE'] = c['OFF_CG'] + C
    c['N_IN'] = c['OFF_MERGE'] + 3 * D
    c['TE'] = T + 2 * HB
    return c


def splits(total, maxsz):
    n = -(-total // maxsz)
    base = -(-total // n)
    base = -(-base // 128) * 128
    out = []
    s = 0
    while s < total:
        e = min(total, s + base)
        out.append((s, e - s))
        s = e
    return out


ARENA_BYTES = 212736


class Ctx:
    def __init__(self, nc):
        self.nc = nc
        self.P = Prog(nc)
        self.ps = [nc.alloc_psum_tensor("psb%d" % i, [128, 512], F32) for i in range(8)]
        self.rr = {}
        self.nbuf = 0
        self.arena = nc.alloc_sbuf_tensor("arena", [128, ARENA_BYTES // 2], BF16)
        self.off = 0
        self.base = 0

    def psum(self, pool):
        i = self.rr.get(pool, 0)
        self.rr[pool] = i + 1
        b = pool[i % len(pool)]
        return self.ps[b], 'psb%d' % b

    def sb(self, shape, dt, name=None):
        esz = 4 if dt == F32 else 2
        n = 1
        for d in shape[1:]:
            n *= d
        nbytes = -(-(n * esz) // 64) * 64
        assert self.off + nbytes <= ARENA_BYTES, ("SBUF arena overflow", name, self.off, nbytes)
        a = self.arena[:, self.off // 2:(self.off + n * esz) // 2]
        self.off += nbytes
        if dt == F32:
            a = a.bitcast(F32)
        if len(shape) == 3:
            a = a.rearrange("p (a b) -> p a b", a=shape[1])
        return a

    def persist(self):
        self.base = self.off

    def phase(self):
        self.peak = max(getattr(self, "peak", 0), self.off)
        self.off = self.base
        self.P.fence()


class Rot:
    def __init__(self, cx, n, shape, dt, name):
        self.b = [cx.sb(shape, dt, "%s%d" % (name, i)) for i in range(n)]
        self.k = ["%s%d" % (name, i) for i in range(n)]
        self.i = 0

    def next(self):
        j = self.i % len(self.b)
        self.i += 1
        return self.b[j], self.k[j]


def load_w(cx, slot, key, W, k0, kdim, c0, ncols, q='pool'):
    KC = kdim // 128
    src = W[k0:k0 + kdim, c0:c0 + ncols].rearrange("(kc p) n -> p kc n", p=128)
    cx.P.dma(q, slot[:, 0:KC, 0:ncols], src, writes=[key])


def rstd_from_sumsq(cx, dst, dkey, src_ap, skey, n, extra_reads=()):
    P = cx.P
    P.op('act', 'activation', dict(out=dst, in_=src_ap, func=AF.Sqrt, scale=1.0 / n, bias=cx.epsb[0:dst.shape[0], :]),
         reads=[skey, 'epsb'] + list(extra_reads), writes=[dkey])
    P.op('dve', 'reciprocal', dict(out=dst, in_=dst), reads=[dkey], writes=[dkey])


def consts(cx, ident_d):
    nc, P = cx.nc, cx.P
    cx.ident = cx.sb([128, 128], F32, "ident")
    P.dma('sp', cx.ident[:, :], ident_d[:, :], writes=['ident'])
    cx.ones_f = cx.sb([128, 128], F32, "ones_f")
    cx.ones_b = cx.sb([128, 128], BF16, "ones_b")
    cx.epsb = cx.sb([128, 1], F32, "epsb")
    P.op('dve', 'memset', dict(ap=cx.ones_f[:, :], constant=1.0), writes=['ones_f'])
    P.op('dve', 'memset', dict(ap=cx.ones_b[:, :], constant=1.0), writes=['ones_b'])
    P.op('dve', 'memset', dict(ap=cx.epsb[:, :], constant=EPS), writes=['epsb'])


def phase_norm(cx, cfg, x_d, tiles, modsets, hT_d):
    nc, P = cx.nc, cx.P
    D = cfg['D']
    KC = D // 128
    xt = Rot(cx, 2, [128, D], F32, "xt")
    junk = cx.sb([128, D], BF16, "junk")
    ss = Rot(cx, 2, [128, 1], F32, "ss")
    hs = Rot(cx, 2, [128, KC, 128], BF16, "hs")
    npb = 4
    for (r0, sz, ms, col0) in tiles:
        x_t, xk = xt.next()
        s_t, sk = ss.next()
        h_t, hk = hs.next()
        G, Sh, gk, shk = modsets[ms]
        P.dma('sp', x_t[0:sz, :], x_d[r0:r0 + sz, :], writes=[xk])
        P.op('act', 'activation', dict(out=junk[0:sz, :], in_=x_t[0:sz, :], func=AF.Square, accum_out=s_t[0:sz, :]),
             reads=[xk], writes=['junk', sk])
        rstd_from_sumsq(cx, s_t[0:sz, :], sk, s_t[0:sz, :], sk, D)
        P.op('dve', 'tensor_scalar', dict(out=x_t[0:sz, :], in0=x_t[0:sz, :], scalar1=s_t[0:sz, 0:1], scalar2=None,
                                          op0=ALU.mult), reads=[xk, sk], writes=[xk])
        for g in range(0, KC, npb):
            pt, pk = cx.psum((4, 5, 6, 7))
            for j in range(min(npb, KC - g)):
                kc = g + j
                P.op('pe', 'transpose', dict(out=pt[:, j * 128:j * 128 + sz], in_=x_t[0:sz, kc * 128:(kc + 1) * 128],
                                             identity=cx.ident[0:sz, 0:sz]), reads=[xk, 'ident'], writes=[pk])
            for j in range(min(npb, KC - g)):
                kc = g + j
                if kc % 2 == 0:
                    P.op('act', 'activation', dict(out=h_t[:, kc, 0:sz], in_=pt[:, j * 128:j * 128 + sz], func=AF.Identity,
                                                   scale=G[:, kc:kc + 1], bias=Sh[:, kc:kc + 1]),
                         reads=[pk, gk, shk], writes=[hk])
                else:
                    P.op('dve', 'tensor_scalar', dict(out=h_t[:, kc, 0:sz], in0=pt[:, j * 128:j * 128 + sz],
                                                      scalar1=G[:, kc:kc + 1], scalar2=Sh[:, kc:kc + 1],
                                                      op0=ALU.mult, op1=ALU.add),
                         reads=[pk, gk, shk], writes=[hk])
        P.dma('sp', hT_d[:, col0:col0 + sz].rearrange("(kc p) t -> p kc t", p=128), h_t[:, :, 0:sz],
              reads=[hk], writes=['hT_d'])


def make_mod(cx, cfg, g_d, sc_d, sh_d, name):
    P = cx.P
    KC = cfg['D'] // 128
    G = cx.sb([128, KC], F32, "G" + name)
    Sh = cx.sb([128, KC], F32, "Sh" + name)
    gt = cx.sb([128, KC], F32, "gt" + name)
    P.dma('sp', G[:, :], sc_d, writes=['G' + name])
    P.dma('sp', gt[:, :], g_d, writes=['gt' + name])
    P.dma('sp', Sh[:, :], sh_d, writes=['Sh' + name])
    P.op('dve', 'scalar_tensor_tensor', dict(out=G[:, :], in0=G[:, :], scalar=1.0, in1=gt[:, :], op0=ALU.add,
                                             op1=ALU.mult), reads=['G' + name, 'gt' + name], writes=['G' + name])
    return (G, Sh, 'G' + name, 'Sh' + name)


def wgroups(cols, wgroup):
    groups = []
    for (c0, n, tag) in cols:
        if groups and groups[-1][0] + groups[-1][1] == c0 and groups[-1][1] + n <= wgroup:
            groups[-1][1] += n
            groups[-1][2].append((c0, n, tag))
        else:
            groups.append([c0, n, [(c0, n, tag)]])
    return groups


def mm_acc(cx, pt_ap, pk, lhs_fn, rhs_fn, KC, reads):
    for kc in range(KC):
        cx.P.op('pe', 'matmul', dict(out=pt_ap, lhsT=lhs_fn(kc), rhs=rhs_fn(kc), start=(kc == 0), stop=(kc == KC - 1)),
                reads=reads, writes=[pk])


def gemm_fm(cx, W, kdim, cols, act, akey, ntok, evac, wrot, wgroup=512, psp=(4, 5, 6, 7), tchunk=512, wk0=0):
    KC = kdim // 128
    groups = wgroups(cols, wgroup)
    slots = {}

    def issue(i):
        slot, skey = wrot.next()
        load_w(cx, slot, skey, W, wk0, kdim, groups[i][0], groups[i][1])
        slots[i] = (slot, skey)
    issue(0)
    for gi, (g0, gn, chunks) in enumerate(groups):
        if gi + 1 < len(groups):
            issue(gi + 1)
        slot, skey = slots[gi]
        for (c0, n, tag) in chunks:
            off = c0 - g0
            for (t0, tsz) in splits(ntok, tchunk):
                pt, pk = cx.psum(psp)
                mm_acc(cx, pt[0:n, 0:tsz], pk, lambda kc: slot[:, kc, off:off + n], lambda kc: act[:, kc, t0:t0 + tsz],
                       KC, [skey, akey])
                evac(pt, pk, tag, n, t0, tsz)


def gemm_tm(cx, W, kdim, c0, ncols, act, akey, tiles, evac, wrot, wgroup=512, psp=(4, 5, 6, 7), wk0=0):
    KC = kdim // 128
    groups = [(c0 + a, b) for (a, b) in splits(ncols, wgroup)]
    slots = {}

    def issue(i):
        slot, skey = wrot.next()
        load_w(cx, slot, skey, W, wk0, kdim, groups[i][0], groups[i][1])
        slots[i] = (slot, skey)
    issue(0)
    for gi, (g0, gn) in enumerate(groups):
        if gi + 1 < len(groups):
            issue(gi + 1)
        slot, skey = slots[gi]
        for (t0, tsz) in tiles:
            pt, pk = cx.psum(psp)
            mm_acc(cx, pt[0:tsz, 0:gn], pk, lambda kc: act[:, kc, t0:t0 + tsz], lambda kc: slot[:, kc, 0:gn],
                   KC, [skey, akey])
            evac(pt, pk, g0, gn, t0, tsz)


def colnorm(cx, srcs, skey, n_total, tsz, psp=(4, 5, 6, 7)):
    P = cx.P
    pt, pk = cx.psum(psp)
    for c, src in enumerate(srcs):
        pp = src.shape[0]
        sq, sqk = cx.sqrot.next()
        P.op('act', 'activation', dict(out=sq[0:pp, 0:tsz], in_=src, func=AF.Square), reads=[skey], writes=[sqk])
        P.op('pe', 'matmul', dict(out=pt[:, 0:tsz], lhsT=cx.ones_f[0:pp, :], rhs=sq[0:pp, 0:tsz],
                                  start=(c == 0), stop=(c == len(srcs) - 1)), reads=[sqk, 'ones_f'], writes=[pk])
    r, rk = cx.rsrot.next()
    rstd_from_sumsq(cx, r[:, 0:tsz], rk, pt[:, 0:tsz], pk, n_total)
    return r, rk


def build_pre(cfg, tiles, TP=1024):
    D, F, KVL, ROPE = cfg['D'], cfg['F'], cfg['KVL'], cfg['ROPE']
    KC = D // 128
    Ttot = sum(t[1] for t in tiles)
    nc = bass.Bass("TRN2", target_bir_lowering=False)
    x_d = nc.dram_tensor("x", [Ttot, D], F32, kind="ExternalInput").ap()
    ident_d = nc.dram_tensor("ident", [128, 128], F32, kind="ExternalInput").ap()
    mods_d = nc.dram_tensor("mods", [128, 5 * KC], F32, kind="ExternalInput").ap()
    wu_d = nc.dram_tensor("wu", [D, F], F32, kind="ExternalInput").ap()
    wkv_d = nc.dram_tensor("wkv", [D, KVL + ROPE], F32, kind="ExternalInput").ap()
    kvg_d = nc.dram_tensor("kvg", [128, KVL // 128], F32, kind="ExternalInput").ap()
    u_o = nc.dram_tensor("u_o", [Ttot, F], BF16, kind="ExternalOutput").ap()
    ckv_o = nc.dram_tensor("ckv_o", [KVL, Ttot], BF16, kind="ExternalOutput").ap()
    kr_o = nc.dram_tensor("kr_o", [ROPE, Ttot], F32, kind="ExternalOutput").ap()
    hT_d = nc.dram_tensor("hT_d", [D, Ttot], BF16).ap()
    cx = Ctx(nc)
    P = cx.P
    consts(cx, ident_d)
    m0 = make_mod(cx, cfg, mods_d[:, 0:KC], mods_d[:, KC:2 * KC], mods_d[:, 2 * KC:3 * KC], "0")
    m1 = make_mod(cx, cfg, mods_d[:, 0:KC], mods_d[:, 3 * KC:4 * KC], mods_d[:, 4 * KC:5 * KC], "1")
    kvg = cx.sb([128, KVL // 128], F32, "kvg")
    P.dma('sp', kvg[:, :], kvg_d[:, :], writes=['kvg'])
    tl = []
    col = 0
    for (r0, sz, ms) in tiles:
        tl.append((r0, sz, ms, col))
        col += sz
    cx.persist()
    cx.phase()
    phase_norm(cx, cfg, x_d, tl, [m0, m1], hT_d)
    cx.phase()
    hT = cx.sb([128, KC, TP], BF16, "hTp")
    wrot = Rot(cx, 2, [128, KC, 512], BF16, "wsl")
    KVC = KVL // 128
    ckv = cx.sb([128, KVC, TP], F32, "ckv")
    cx.sqrot = Rot(cx, 2, [128, 512], F32, "sq")
    cx.rsrot = Rot(cx, 2, [128, 512], F32, "rs")
    st_bf = Rot(cx, 3, [128, 512], BF16, "stb")
    st_f = Rot(cx, 2, [128, 512], F32, "stf")
    for (p0, tp) in splits(Ttot, TP):
        P.dma('sp', hT[:, :, 0:tp], hT_d[:, p0:p0 + tp].rearrange("(kc p) t -> p kc t", p=128),
              reads=['hT_d'], writes=['hTp'])

        def evac_kv(pt, pk, tag, n, t0, tsz, p0=p0):
            if tag < KVC:
                P.op('act', 'activation', dict(out=ckv[0:n, tag, t0:t0 + tsz], in_=pt[0:n, 0:tsz], func=AF.Copy),
                     reads=[pk], writes=['ckv'])
            else:
                s, sk = st_f.next()
                P.op('dve', 'tensor_copy', dict(out=s[0:n, 0:tsz], in_=pt[0:n, 0:tsz]), reads=[pk], writes=[sk])
                P.dma('sp', kr_o[:, p0 + t0:p0 + t0 + tsz], s[0:n, 0:tsz], reads=[sk], writes=['kr_o'])
        cols = [(i * 128, 128, i) for i in range(KVC)] + [(KVL, ROPE, KVC)]
        gemm_fm(cx, wkv_d, D, cols, hT, 'hTp', tp, evac_kv, wrot)
        for (t0, tsz) in splits(tp, 512):
            r, rk = colnorm(cx, [ckv[:, c, t0:t0 + tsz] for c in range(KVC)], 'ckv', KVL, tsz)
            for c in range(KVC):
                s, sk = st_bf.next()
                P.op('dve', 'scalar_tensor_tensor', dict(out=s[:, 0:tsz], in0=ckv[:, c, t0:t0 + tsz], scalar=kvg[:, c:c + 1],
                                                         in1=r[:, 0:tsz], op0=ALU.mult, op1=ALU.mult),
                     reads=['ckv', 'kvg', rk], writes=[sk])
                P.dma('sp', ckv_o[c * 128:(c + 1) * 128, p0 + t0:p0 + t0 + tsz], s[:, 0:tsz], reads=[sk], writes=['ckv_o'])

        def evac_u(pt, pk, g0, gn, t0, tsz, p0=p0):
            s, sk = st_bf.next()
            P.op('act', 'activation', dict(out=s[0:tsz, 0:gn], in_=pt[0:tsz, 0:gn], func=AF.Copy), reads=[pk], writes=[sk])
            P.dma('sp', u_o[p0 + t0:p0 + t0 + tsz, g0:g0 + gn], s[0:tsz, 0:gn], reads=[sk], writes=['u_o'])
        gemm_tm(cx, wu_d, D, 0, F, hT, 'hTp', splits(tp, 128), evac_u, wrot)
    P.finish_wait('sp', ['u_o', 'ckv_o', 'kr_o'])
    P.emit()
    return nc


def block_io(nc, cfg, pfx, W=None):
    D, F, QL, KVL, NH, NOPE, ROPE, VD, C, CW = [cfg[k] for k in ('D', 'F', 'QL', 'KVL', 'NH', 'NOPE', 'ROPE', 'VD', 'C', 'CW')]
    T, HB, NK, NS, TE, MLA, QK, N_IN = [cfg[k] for k in ('T', 'HB', 'NK', 'NS', 'TE', 'MLA', 'QK', 'N_IN')]
    KC, QLC, CC = D // 128, QL // 128, C // 128
    di = lambda n, s, dt=F32: nc.dram_tensor(pfx + n, list(s), dt, kind="ExternalInput").ap()
    A = {}
    A['x'] = di("x", [TE, D])
    A['mods'] = di("mods", [128, 3 * KC])
    A['gate'] = di("gate", [1, D])
    A['hmask'] = di("hmask", [1, TE])
    A['ckvn'] = di("ckvn", [KVL, NK], BF16)
    A['kr'] = di("kr", [ROPE, NK])
    A['u_all'] = di("u_all", [NS, F], BF16)
    A['ropeq'] = di("ropeq", [ROPE, 2 * T])
    A['ropek'] = di("ropek", [ROPE, 2 * NK])
    A['cst'] = di("cst", [NS, T], BF16)
    A['sst'] = di("sst", [NS, T], BF16)
    A['cc'] = di("cc", [F, F], BF16)
    A['nsc'] = di("nsc", [F, F], BF16)
    A['out'] = nc.dram_tensor(pfx + "out", [T, D], F32, kind="ExternalOutput").ap()
    ds = lambda n, s, dt: nc.dram_tensor(pfx + n, list(s), dt).ap()
    A['hT_d'] = ds("hT_d", [D, TE], BF16)
    A['gf_d'] = ds("gf_d", [F, TE], BF16)
    A['cqn_d'] = ds("cqn_d", [QL, TE], BF16)
    A['gm_d'] = ds("gm_d", [MLA, TE], BF16)
    A['uc_d'] = ds("uc_d", [C, TE], F32)
    A['gc_d'] = ds("gc_d", [C, TE], BF16)
    A['sr_d'] = ds("sr_d", [128, NK], F32)
    A['krf_d'] = ds("krf_d", [ROPE, NK], F32)
    A['ym_d'] = ds("ym_d", [MLA, T], BF16)
    A['yf_d'] = ds("yf_d", [F, T], BF16)
    A['yc_d'] = ds("yc_d", [C, T], BF16)
    A['mg_d'] = ds("mg_d", [D, T], BF16)
    return A


def shared_io(nc, cfg):
    D, F, QL, KVL, NH, NOPE, ROPE, VD, C, CW = [cfg[k] for k in ('D', 'F', 'QL', 'KVL', 'NH', 'NOPE', 'ROPE', 'VD', 'C', 'CW')]
    MLA, QK, N_IN = cfg['MLA'], cfg['QK'], cfg['N_IN']
    QLC, CC = QL // 128, C // 128
    di = lambda n, s, dt=F32: nc.dram_tensor(n, list(s), dt, kind="ExternalInput").ap()
    W = {}
    W['ident'] = di("ident", [128, 128])
    W['rmt'] = di("rmt", [ROPE, ROPE])
    W['vecs'] = di("vecs", [128, QLC + 4 + 3 * CC + CC * CW])
    W['w_in'] = di("w_in", [D, N_IN])
    W['w_uq'] = di("w_uq", [QL, NH * QK])
    W['w_ukv'] = di("w_ukv", [KVL, NH * (NOPE + VD)])
    W['w_fnet'] = di("w_fnet", [F, F])
    W['w_pw2'] = di("w_pw2", [C, C])
    W['w_br_f'] = di("w_br_f", [F, D])
    W['w_br_m'] = di("w_br_m", [MLA, D])
    W['w_br_c'] = di("w_br_c", [C, D])
    W['w_out'] = di("w_out", [D, D])
    return W


def build_block(cfgs, TPB=768, TPF=512):
    nc = bass.Bass("TRN2", target_bir_lowering=False)
    W = shared_io(nc, cfgs[0][0])
    cx = Ctx(nc)
    consts(cx, W['ident'])
    cx.base0 = cx.off
    outs = []
    for (cfg, pfx) in cfgs:
        A = block_io(nc, cfg, pfx)
        cx.off = cx.base0
        cx.P.fence()
        emit_block(cx, cfg, A, W, TPB, TPF)
        outs.append('out')
    cx.P.finish_wait('sp', ['out'])
    cx.P.emit()
    return nc


def emit_block(cx, cfg, A, W, TPB, TPF):
    nc = cx.nc
    D, F, QL, KVL, NH, NOPE, ROPE, VD, C, CW = [cfg[k] for k in ('D', 'F', 'QL', 'KVL', 'NH', 'NOPE', 'ROPE', 'VD', 'C', 'CW')]
    T, HB, NK, NS, TE, MLA, QK, N_IN = [cfg[k] for k in ('T', 'HB', 'NK', 'NS', 'TE', 'MLA', 'QK', 'N_IN')]
    KC, FC, QLC, KVC, CC, MC = D // 128, F // 128, QL // 128, KVL // 128, C // 128, MLA // 128
    NSC = NS // 128
    NKB = NK // 128
    SCALE = QK ** -0.5
    TPB = min(TPB, TE)
    TPF = min(TPF, T)
    x_d, mods_d, gate_d, hmask_d, ckvn_d, kr_d, u_d = [A[k] for k in ('x', 'mods', 'gate', 'hmask', 'ckvn', 'kr', 'u_all')]
    ropeq_d, ropek_d, cst_d, sst_d, cc_d, nsc_d, out_d = [A[k] for k in ('ropeq', 'ropek', 'cst', 'sst', 'cc', 'nsc', 'out')]
    hT_d, gf_d, cqn_d, gm_d, uc_d, gc_d, sr_d, krf_d, ym_d, yf_d, yc_d, mg_d = [A[k] for k in (
        'hT_d', 'gf_d', 'cqn_d', 'gm_d', 'uc_d', 'gc_d', 'sr_d', 'krf_d', 'ym_d', 'yf_d', 'yc_d', 'mg_d')]
    rmt_d, vec_d = W['rmt'], W['vecs']
    w_in, w_uq, w_ukv, w_fnet, w_pw2, w_brf, w_brm, w_brc, w_out = [W[k] for k in (
        'w_in', 'w_uq', 'w_ukv', 'w_fnet', 'w_pw2', 'w_br_f', 'w_br_m', 'w_br_c', 'w_out')]
    NV = QLC + 4 + 3 * CC + CC * CW
    P = cx.P
    GP = (4, 5, 6, 7)
    m0 = make_mod(cx, cfg, mods_d[:, 0:KC], mods_d[:, KC:2 * KC], mods_d[:, 2 * KC:3 * KC], "0")
    vec = cx.sb([128, NV], F32, "vec")
    P.dma('sp', vec[:, :], vec_d[:, :], writes=['vec'])
    o = 0
    qan = vec[:, o:o + QLC]; o += QLC
    qn_n = vec[:, o:o + 1]; qn_r = vec[0:ROPE, o + 1:o + 2]; kn_n = vec[:, o + 2:o + 3]; kn_r = vec[0:ROPE, o + 3:o + 4]; o += 4
    conv_b = vec[:, o:o + CC]; o += CC
    cln_g = vec[:, o:o + CC]; o += CC
    cln_b = vec[:, o:o + CC]; o += CC
    conv_w = vec[:, o:o + CC * CW]; o += CC * CW
    P.op('dve', 'tensor_scalar', dict(out=vec[:, QLC:QLC + 2], in0=vec[:, QLC:QLC + 2], scalar1=SCALE, scalar2=None,
                                      op0=ALU.mult), reads=['vec'], writes=['vec'])
    rmt = cx.sb([128, ROPE], F32, "rmt")
    P.dma('sp', rmt[0:ROPE, :], rmt_d[:, :], writes=['rmt'])
    cx.sqrot = Rot(cx, 2, [128, 512], F32, "sq")
    cx.rsrot = Rot(cx, 2, [128, 512], F32, "rs")
    st_bf = Rot(cx, 3, [128, 512], BF16, "stb")
    st_f = Rot(cx, 3, [128, 512], F32, "stf")
    cx.persist()

    cx.phase()
    tl = []
    for (t0, sz) in splits(TE, 128):
        tl.append((t0, sz, 0, t0))
    phase_norm(cx, cfg, x_d, tl, [m0], hT_d)

    cx.phase()
    hT = cx.sb([128, KC, TPB], BF16, "hTp")
    wrot = Rot(cx, 2, [128, KC, 512], BF16, "wsl")
    cq = cx.sb([128, QLC, TPB], F32, "cq")
    a_sb = cx.sb([128, CC, TPB], F32, "a_sb")
    mask = cx.sb([128, TE], F32, "mask")
    P.dma('sp', mask[:, :], hmask_d[0:1, :].broadcast_to([128, TE]), writes=['mask'])
    cols = []
    for i in range(FC):
        cols.append((cfg['OFF_FG'] + i * 128, 128, ('fg', i)))
    for i in range(QLC):
        cols.append((cfg['OFF_Q'] + i * 128, 128, ('q', i)))
    for i in range(MC):
        cols.append((cfg['OFF_MG'] + i * 128, 128, ('mg', i)))
    for i in range(CC):
        cols.append((cfg['OFF_GLU'] + i * 128, 128, ('ga', i)))
    for i in range(CC):
        cols.append((cfg['OFF_GLU'] + C + i * 128, 128, ('gg', i)))
    for i in range(CC):
        cols.append((cfg['OFF_CG'] + i * 128, 128, ('cg', i)))
    for (p0, tp) in splits(TE, TPB):
        P.dma('sp', hT[:, :, 0:tp], hT_d[:, p0:p0 + tp].rearrange("(kc p) t -> p kc t", p=128),
              reads=['hT_d'], writes=['hTp'])

        def evac_b(pt, pk, tag, n, t0, tsz, p0=p0):
            kind, i = tag
            if kind in ('fg', 'mg', 'cg'):
                dst = {'fg': gf_d, 'mg': gm_d, 'cg': gc_d}[kind]
                s, sk = st_bf.next()
                P.op('act', 'activation', dict(out=s[:, 0:tsz], in_=pt[:, 0:tsz], func=AF.Silu), reads=[pk], writes=[sk])
                P.dma('sp', dst[i * 128:(i + 1) * 128, p0 + t0:p0 + t0 + tsz], s[:, 0:tsz], reads=[sk],
                      writes=[kind + '_d'])
            elif kind == 'q':
                P.op('act', 'activation', dict(out=cq[:, i, t0:t0 + tsz], in_=pt[:, 0:tsz], func=AF.Copy),
                     reads=[pk], writes=['cq'])
            elif kind == 'ga':
                P.op('dve', 'tensor_copy', dict(out=a_sb[:, i, t0:t0 + tsz], in_=pt[:, 0:tsz]), reads=[pk], writes=['a_sb'])
            else:
                s, sk = st_f.next()
                P.op('act', 'activation', dict(out=s[:, 0:tsz], in_=pt[:, 0:tsz], func=AF.Sigmoid), reads=[pk], writes=[sk])
                P.op('dve', 'tensor_tensor', dict(out=s[:, 0:tsz], in0=s[:, 0:tsz], in1=a_sb[:, i, t0:t0 + tsz], op=ALU.mult),
                     reads=[sk, 'a_sb'], writes=[sk])
                P.op('pool', 'tensor_tensor', dict(out=s[:, 0:tsz], in0=s[:, 0:tsz], in1=mask[:, p0 + t0:p0 + t0 + tsz],
                                                   op=ALU.mult), reads=[sk, 'mask'], writes=[sk])
                P.dma('sp', uc_d[i * 128:(i + 1) * 128, p0 + t0:p0 + t0 + tsz], s[:, 0:tsz], reads=[sk], writes=['uc_d'])
        gemm_fm(cx, w_in, D, cols, hT, 'hTp', tp, evac_b, wrot, psp=GP)
        for (t0, tsz) in splits(tp, 512):
            r, rk = colnorm(cx, [cq[:, c, t0:t0 + tsz] for c in range(QLC)], 'cq', QL, tsz, psp=GP)
            for c in range(QLC):
                s, sk = st_bf.next()
                P.op('dve', 'scalar_tensor_tensor', dict(out=s[:, 0:tsz], in0=cq[:, c, t0:t0 + tsz], scalar=qan[:, c:c + 1],
                                                         in1=r[:, 0:tsz], op0=ALU.mult, op1=ALU.mult),
                     reads=['cq', 'vec', rk], writes=[sk])
                P.dma('sp', cqn_d[c * 128:(c + 1) * 128, p0 + t0:p0 + t0 + tsz], s[:, 0:tsz], reads=[sk], writes=['cqn_d'])

    cx.phase()
    KCH = splits(NK, 512)
    tmpf = Rot(cx, 3, [128, 512], F32, "tmpf")
    markC = cx.off
    krt = Rot(cx, 2, [128, 512], F32, "krt")
    tab = Rot(cx, 2, [128, 2, 512], F32, "tab")
    for (k0, ksz) in KCH:
        kt, kk = krt.next()
        tb, tk = tab.next()
        P.dma('sp', kt[0:ROPE, 0:ksz], kr_d[:, k0:k0 + ksz], writes=[kk])
        P.dma('sp', tb[0:ROPE, :, 0:ksz], ropek_d.rearrange("p (a k) -> p a k", a=2)[:, :, k0:k0 + ksz], writes=[tk])
        sq, sqk = cx.sqrot.next()
        P.op('act', 'activation', dict(out=sq[0:ROPE, 0:ksz], in_=kt[0:ROPE, 0:ksz], func=AF.Square), reads=[kk], writes=[sqk])
        pt, pk = cx.psum(GP)
        P.op('pe', 'matmul', dict(out=pt[:, 0:ksz], lhsT=cx.ones_f[0:ROPE, :], rhs=sq[0:ROPE, 0:ksz], start=True, stop=True),
             reads=[sqk, 'ones_f'], writes=[pk])
        s, sk = st_f.next()
        P.op('act', 'activation', dict(out=s[:, 0:ksz], in_=pt[:, 0:ksz], func=AF.Copy), reads=[pk], writes=[sk])
        P.dma('sp', sr_d[:, k0:k0 + ksz], s[:, 0:ksz], reads=[sk], writes=['sr_d'])
        P.op('dve', 'tensor_scalar', dict(out=kt[0:ROPE, 0:ksz], in0=kt[0:ROPE, 0:ksz], scalar1=kn_r, scalar2=None,
                                          op0=ALU.mult), reads=[kk, 'vec', sqk], writes=[kk])
        pt2, pk2 = cx.psum(GP)
        P.op('pe', 'matmul', dict(out=pt2[0:ROPE, 0:ksz], lhsT=rmt[0:ROPE, :], rhs=kt[0:ROPE, 0:ksz], start=True, stop=True),
             reads=[kk, 'rmt'], writes=[pk2])
        t2, t2k = tmpf.next()
        P.op('dve', 'tensor_tensor', dict(out=t2[0:ROPE, 0:ksz], in0=pt2[0:ROPE, 0:ksz], in1=tb[0:ROPE, 1, 0:ksz], op=ALU.mult),
             reads=[pk2, tk], writes=[t2k])
        P.op('pool', 'tensor_tensor', dict(out=kt[0:ROPE, 0:ksz], in0=kt[0:ROPE, 0:ksz], in1=tb[0:ROPE, 0, 0:ksz], op=ALU.mult),
             reads=[kk, tk, pk2], writes=[kk])
        P.op('dve', 'tensor_tensor', dict(out=t2[0:ROPE, 0:ksz], in0=t2[0:ROPE, 0:ksz], in1=kt[0:ROPE, 0:ksz], op=ALU.add),
             reads=[t2k, kk], writes=[t2k])
        P.dma('sp', krf_d[:, k0:k0 + ksz], t2[0:ROPE, 0:ksz], reads=[t2k], writes=['krf_d'])
    cx.off = markC
    P.fence()
    cqn = cx.sb([128, QLC, T], BF16, "cqn")
    P.dma('sp', cqn[:, :, :], cqn_d[:, HB:HB + T].rearrange("(c p) t -> p c t", p=128), reads=['cqn_d'], writes=['cqn'])
    rqt = Rot(cx, 2, [128, 2, 512], F32, "rqt")
    Kn = Rot(cx, 2, [128, NK], BF16, "Kn")
    Kr = Rot(cx, 2, [128, NK], BF16, "Kr")
    Vh = Rot(cx, 2, [128, NKB, VD], BF16, "Vh")
    Qn = Rot(cx, 2, [128, T], BF16, "Qn")
    Qr = Rot(cx, 2, [128, T], BF16, "Qr")
    wq = Rot(cx, 2, [128, QLC, QK], BF16, "wq")
    wkv = Rot(cx, 2, [128, KVC, NOPE + VD], BF16, "wkvh")
    ckt = Rot(cx, 2, [128, KVC, 512], BF16, "ckt")
    srt = Rot(cx, 2, [128, 512], F32, "srt")
    kft = Rot(cx, 2, [128, 512], F32, "kft")
    PT = Rot(cx, 3, [128, 512], BF16, "PT")
    gmt = Rot(cx, 2, [128, 512], BF16, "gmt")
    QCH = splits(T, 512)
    for h in range(NH):
        wq_h, wqk = wq.next()
        wkv_h, wkvk = wkv.next()
        load_w(cx, wq_h, wqk, w_uq, 0, QL, h * QK, QK)
        load_w(cx, wkv_h, wkvk, w_ukv, 0, KVL, h * (NOPE + VD), NOPE + VD)
        Kn_h, Knk = Kn.next()
        Kr_h, Krk = Kr.next()
        V_h, Vk = Vh.next()
        Qn_h, Qnk = Qn.next()
        Qr_h, Qrk = Qr.next()
        for (k0, ksz) in KCH:
            ck, ckk = ckt.next()
            P.dma('sp', ck[:, :, 0:ksz], ckvn_d[:, k0:k0 + ksz].rearrange("(c p) k -> p c k", p=128), writes=[ckk])
            sr, srk = srt.next()
            P.dma('sp', sr[:, 0:ksz], sr_d[:, k0:k0 + ksz], reads=['sr_d'], writes=[srk])
            kf, kfk = kft.next()
            P.dma('sp', kf[0:ROPE, 0:ksz], krf_d[:, k0:k0 + ksz], reads=['krf_d'], writes=[kfk])
            pt, pk = cx.psum(GP)
            mm_acc(cx, pt[:, 0:ksz], pk, lambda c: wkv_h[:, c, 0:NOPE], lambda c: ck[:, c, 0:ksz], KVC, [wkvk, ckk])
            sq, sqk = cx.sqrot.next()
            P.op('act', 'activation', dict(out=sq[:, 0:ksz], in_=pt[:, 0:ksz], func=AF.Square), reads=[pk], writes=[sqk])
            ps2, pk2 = cx.psum(GP)
            P.op('pe', 'matmul', dict(out=ps2[:, 0:ksz], lhsT=cx.ones_f[:, :], rhs=sq[:, 0:ksz], start=True, stop=True),
                 reads=[sqk, 'ones_f'], writes=[pk2])
            r, rk = cx.rsrot.next()
            P.op('dve', 'tensor_tensor', dict(out=r[:, 0:ksz], in0=ps2[:, 0:ksz], in1=sr[:, 0:ksz], op=ALU.add),
                 reads=[pk2, srk], writes=[rk])
            rstd_from_sumsq(cx, r[:, 0:ksz], rk, r[:, 0:ksz], rk, QK)
            P.op('dve', 'scalar_tensor_tensor', dict(out=Kn_h[:, k0:k0 + ksz], in0=pt[:, 0:ksz], scalar=kn_n, in1=r[:, 0:ksz],
                                                     op0=ALU.mult, op1=ALU.mult), reads=[pk, 'vec', rk], writes=[Knk])
            P.op('pool', 'tensor_tensor', dict(out=Kr_h[0:ROPE, k0:k0 + ksz], in0=kf[0:ROPE, 0:ksz], in1=r[0:ROPE, 0:ksz],
                                               op=ALU.mult), reads=[kfk, rk], writes=[Krk])
            pv, pvk = cx.psum(GP)
            nsub = ksz // 128
            for j in range(nsub):
                mm_acc(cx, pv[:, j * VD:(j + 1) * VD], pvk, lambda c: ck[:, c, j * 128:(j + 1) * 128],
                       lambda c: wkv_h[:, c, NOPE:NOPE + VD], KVC, [wkvk, ckk])
            P.op('act', 'activation', dict(out=V_h[:, k0 // 128:k0 // 128 + nsub, :],
                                           in_=pv[:, 0:nsub * VD].rearrange("p (a b) -> p a b", a=nsub), func=AF.Copy),
                 reads=[pvk], writes=[Vk])
        for (q0, qsz) in QCH:
            pn, pnk = cx.psum(GP)
            mm_acc(cx, pn[:, 0:qsz], pnk, lambda c: wq_h[:, c, 0:NOPE], lambda c: cqn[:, c, q0:q0 + qsz], QLC, [wqk, 'cqn'])
            pr, prk = cx.psum(GP)
            mm_acc(cx, pr[0:ROPE, 0:qsz], prk, lambda c: wq_h[:, c, NOPE:QK], lambda c: cqn[:, c, q0:q0 + qsz], QLC,
                   [wqk, 'cqn'])
            pss, pssk = cx.psum(GP)
            sq, sqk = cx.sqrot.next()
            P.op('act', 'activation', dict(out=sq[:, 0:qsz], in_=pn[:, 0:qsz], func=AF.Square), reads=[pnk], writes=[sqk])
            P.op('pe', 'matmul', dict(out=pss[:, 0:qsz], lhsT=cx.ones_f[:, :], rhs=sq[:, 0:qsz], start=True, stop=False),
                 reads=[sqk, 'ones_f'], writes=[pssk])
            sq2, sq2k = cx.sqrot.next()
            P.op('act', 'activation', dict(out=sq2[0:ROPE, 0:qsz], in_=pr[0:ROPE, 0:qsz], func=AF.Square), reads=[prk],
                 writes=[sq2k])
            P.op('pe', 'matmul', dict(out=pss[:, 0:qsz], lhsT=cx.ones_f[0:ROPE, :], rhs=sq2[0:ROPE, 0:qsz], start=False,
                                      stop=True), reads=[sq2k, 'ones_f'], writes=[pssk])
            r, rk = cx.rsrot.next()
            rstd_from_sumsq(cx, r[:, 0:qsz], rk, pss[:, 0:qsz], pssk, QK)
            P.op('dve', 'scalar_tensor_tensor', dict(out=Qn_h[:, q0:q0 + qsz], in0=pn[:, 0:qsz], scalar=qn_n, in1=r[:, 0:qsz],
                                                     op0=ALU.mult, op1=ALU.mult), reads=[pnk, 'vec', rk], writes=[Qnk])
            rq, rqk = rqt.next()
            P.dma('sp', rq[0:ROPE, :, 0:qsz], ropeq_d.rearrange("p (a k) -> p a k", a=2)[:, :, q0:q0 + qsz], writes=[rqk])
            qg, qgk = tmpf.next()
            P.op('act', 'activation', dict(out=qg[0:ROPE, 0:qsz], in_=pr[0:ROPE, 0:qsz], func=AF.Identity, scale=qn_r),
                 reads=[prk, 'vec'], writes=[qgk])
            prot, protk = cx.psum(GP)
            P.op('pe', 'matmul', dict(out=prot[0:ROPE, 0:qsz], lhsT=rmt[0:ROPE, :], rhs=qg[0:ROPE, 0:qsz], start=True, stop=True),
                 reads=[qgk, 'rmt'], writes=[protk])
            t2, t2k = tmpf.next()
            P.op('dve', 'tensor_tensor', dict(out=t2[0:ROPE, 0:qsz], in0=prot[0:ROPE, 0:qsz], in1=rq[0:ROPE, 1, 0:qsz],
                                              op=ALU.mult), reads=[protk, rqk], writes=[t2k])
            P.op('pool', 'tensor_tensor', dict(out=qg[0:ROPE, 0:qsz], in0=qg[0:ROPE, 0:qsz], in1=rq[0:ROPE, 0, 0:qsz],
                                               op=ALU.mult), reads=[qgk, rqk, protk], writes=[qgk])
            P.op('dve', 'tensor_tensor', dict(out=t2[0:ROPE, 0:qsz], in0=t2[0:ROPE, 0:qsz], in1=qg[0:ROPE, 0:qsz], op=ALU.add),
                 reads=[t2k, qgk], writes=[t2k])
            P.op('dve', 'tensor_tensor', dict(out=Qr_h[0:ROPE, q0:q0 + qsz], in0=t2[0:ROPE, 0:qsz], in1=r[0:ROPE, 0:qsz],
                                              op=ALU.mult), reads=[t2k, rk], writes=[Qrk])
        for (q0, qsz) in QCH:
            po, pok = cx.psum((2,))
            pd, pdk = cx.psum((3,))
            for kb in range(NKB):
                psS, psk = cx.psum((0, 1))
                P.op('pe', 'matmul', dict(out=psS[:, 0:qsz], lhsT=Kn_h[:, kb * 128:(kb + 1) * 128], rhs=Qn_h[:, q0:q0 + qsz],
                                          start=True, stop=False), reads=[Knk, Qnk], writes=[psk])
                P.op('pe', 'matmul', dict(out=psS[:, 0:qsz], lhsT=Kr_h[0:ROPE, kb * 128:(kb + 1) * 128],
                                          rhs=Qr_h[0:ROPE, q0:q0 + qsz], start=False, stop=True), reads=[Krk, Qrk], writes=[psk])
                pT, pTk = PT.next()
                P.op('act', 'activation', dict(out=pT[:, 0:qsz], in_=psS[:, 0:qsz], func=AF.Exp), reads=[psk], writes=[pTk])
                P.op('pe', 'matmul', dict(out=po[:, 0:qsz], lhsT=V_h[:, kb, :], rhs=pT[:, 0:qsz], start=(kb == 0),
                                          stop=(kb == NKB - 1)), reads=[Vk, pTk], writes=[pok])
                P.op('pe', 'matmul', dict(out=pd[:, 0:qsz], lhsT=cx.ones_b[:, :], rhs=pT[:, 0:qsz], start=(kb == 0),
                                          stop=(kb == NKB - 1)), reads=['ones_b', pTk], writes=[pdk])
            rd, rdk = tmpf.next()
            P.op('dve', 'reciprocal', dict(out=rd[:, 0:qsz], in_=pd[:, 0:qsz]), reads=[pdk], writes=[rdk])
            P.op('dve', 'tensor_tensor', dict(out=rd[:, 0:qsz], in0=po[:, 0:qsz], in1=rd[:, 0:qsz], op=ALU.mult),
                 reads=[pok, rdk], writes=[rdk])
            gm, gmk = gmt.next()
            P.dma('sp', gm[:, 0:qsz], gm_d[h * VD:(h + 1) * VD, HB + q0:HB + q0 + qsz], reads=['mg_d'], writes=[gmk])
            s, sk = st_bf.next()
            P.op('dve', 'tensor_tensor', dict(out=s[:, 0:qsz], in0=rd[:, 0:qsz], in1=gm[:, 0:qsz], op=ALU.mult),
                 reads=[rdk, gmk], writes=[sk])
            P.dma('sp', ym_d[h * VD:(h + 1) * VD, q0:q0 + qsz], s[:, 0:qsz], reads=[sk], writes=['ym_d'])

    cx.phase()
    KD = 256
    PTs = cx.sb([128, FC, T], BF16, "PTs")
    QTs = cx.sb([128, FC, T], BF16, "QTs")
    markD = cx.off
    ctab = cx.sb([128, NSC, KD], BF16, "ctab")
    stab = cx.sb([128, NSC, KD], BF16, "stab")
    ut = Rot(cx, 2, [128, NSC, 128], BF16, "ut")
    for (k0, ksz) in splits(T, KD):
        P.dma('sp', ctab[:, :, 0:ksz], cst_d[:, k0:k0 + ksz].rearrange("(s p) k -> p s k", p=128), writes=['ctab'])
        P.dma('sp', stab[:, :, 0:ksz], sst_d[:, k0:k0 + ksz].rearrange("(s p) k -> p s k", p=128), writes=['stab'])
        for c in range(FC):
            u_t, uk = ut.next()
            P.dma('sp', u_t[:, :, :], u_d[:, c * 128:(c + 1) * 128].rearrange("(s p) c -> p s c", p=128), writes=[uk])
            pp, ppk = cx.psum(GP)
            mm_acc(cx, pp[:, 0:ksz], ppk, lambda s: u_t[:, s, :], lambda s: ctab[:, s, 0:ksz], NSC, [uk, 'ctab'])
            P.op('act', 'activation', dict(out=PTs[:, c, k0:k0 + ksz], in_=pp[:, 0:ksz], func=AF.Copy), reads=[ppk], writes=['PTs'])
            pq, pqk = cx.psum(GP)
            mm_acc(cx, pq[:, 0:ksz], pqk, lambda s: u_t[:, s, :], lambda s: stab[:, s, 0:ksz], NSC, [uk, 'stab'])
            P.op('dve', 'tensor_copy', dict(out=QTs[:, c, k0:k0 + ksz], in_=pq[:, 0:ksz]), reads=[pqk], writes=['QTs'])
    cx.off = markD
    P.fence()
    ccs = cx.sb([128, FC, F], BF16, "ccs")
    nscs = cx.sb([128, FC, F], BF16, "nscs")
    wfn = cx.sb([128, FC, F], BF16, "wfn")
    P.dma('sp', ccs[:, :, :], cc_d.rearrange("(c p) n -> p c n", p=128), writes=['ccs'])
    P.dma('sp', nscs[:, :, :], nsc_d.rearrange("(c p) n -> p c n", p=128), writes=['nscs'])
    load_w(cx, wfn, 'wfn', w_fnet, 0, F, 0, F)
    ZT = Rot(cx, 2, [128, FC, 512], BF16, "ZT")
    gft = Rot(cx, 2, [128, 512], BF16, "gft")
    for (t0, tsz) in splits(T, 512):
        z_t, zk = ZT.next()
        for c2 in range(FC):
            pz, pzk = cx.psum(GP)
            for c in range(FC):
                P.op('pe', 'matmul', dict(out=pz[:, 0:tsz], lhsT=ccs[:, c, c2 * 128:(c2 + 1) * 128], rhs=PTs[:, c, t0:t0 + tsz],
                                          start=(c == 0), stop=False), reads=['ccs', 'PTs'], writes=[pzk])
            for c in range(FC):
                P.op('pe', 'matmul', dict(out=pz[:, 0:tsz], lhsT=nscs[:, c, c2 * 128:(c2 + 1) * 128], rhs=QTs[:, c, t0:t0 + tsz],
                                          start=False, stop=(c == FC - 1)), reads=['nscs', 'QTs'], writes=[pzk])
            P.op('act', 'activation', dict(out=z_t[:, c2, 0:tsz], in_=pz[:, 0:tsz], func=AF.Copy), reads=[pzk], writes=[zk])
        for n in range(FC):
            py, pyk = cx.psum(GP)
            mm_acc(cx, py[:, 0:tsz], pyk, lambda c: wfn[:, c, n * 128:(n + 1) * 128], lambda c: z_t[:, c, 0:tsz], FC,
                   ['wfn', zk])
            g_t, gk = gft.next()
            P.dma('sp', g_t[:, 0:tsz], gf_d[n * 128:(n + 1) * 128, HB + t0:HB + t0 + tsz], reads=['fg_d'], writes=[gk])
            s, sk = st_bf.next()
            P.op('dve', 'tensor_tensor', dict(out=s[:, 0:tsz], in0=py[:, 0:tsz], in1=g_t[:, 0:tsz], op=ALU.mult),
                 reads=[pyk, gk], writes=[sk])
            P.dma('sp', yf_d[n * 128:(n + 1) * 128, t0:t0 + tsz], s[:, 0:tsz], reads=[sk], writes=['yf_d'])

    cx.phase()
    cv = cx.sb([128, CC, T], F32, "cv")
    uct = Rot(cx, 2, [128, TE], F32, "uct")
    half = CW // 2
    for c in range(CC):
        u_t, uk = uct.next()
        P.dma('sp', u_t[:, :], uc_d[c * 128:(c + 1) * 128, :], reads=['uc_d'], writes=[uk])
        eng = 'dve'
        ck = 'cv%d' % c
        for j in range(CW):
            o0 = HB + j - half
            wj = conv_w[:, c * CW + j:c * CW + j + 1]
            if j == 0:
                P.op(eng, 'tensor_scalar', dict(out=cv[:, c, :], in0=u_t[:, o0:o0 + T], scalar1=wj, scalar2=conv_b[:, c:c + 1],
                                                op0=ALU.mult, op1=ALU.add), reads=[uk, 'vec'], writes=[ck])
            else:
                P.op(eng, 'scalar_tensor_tensor', dict(out=cv[:, c, :], in0=u_t[:, o0:o0 + T], scalar=wj, in1=cv[:, c, :],
                                                       op0=ALU.mult, op1=ALU.add), reads=[uk, 'vec', ck], writes=[ck])
    cvk = ['cv%d' % c for c in range(CC)]
    wpw = cx.sb([128, CC, C], BF16, "wpw")
    load_w(cx, wpw, 'wpw', w_pw2, 0, C, 0, C)
    ucn = Rot(cx, 2, [128, CC, 512], BF16, "ucn")
    gct = Rot(cx, 2, [128, 512], BF16, "gct")
    m2t = Rot(cx, 2, [128, 512], F32, "m2t")
    for (t0, tsz) in splits(T, 512):
        pm, pmk = cx.psum(GP)
        for c in range(CC):
            P.op('pe', 'matmul', dict(out=pm[:, 0:tsz], lhsT=cx.ones_f[:, :], rhs=cv[:, c, t0:t0 + tsz], start=(c == 0),
                                      stop=(c == CC - 1)), reads=['ones_f'] + cvk, writes=[pmk])
        pe2, pe2k = cx.psum(GP)
        for c in range(CC):
            sq, sqk = cx.sqrot.next()
            P.op('act', 'activation', dict(out=sq[:, 0:tsz], in_=cv[:, c, t0:t0 + tsz], func=AF.Square), reads=cvk, writes=[sqk])
            P.op('pe', 'matmul', dict(out=pe2[:, 0:tsz], lhsT=cx.ones_f[:, :], rhs=sq[:, 0:tsz], start=(c == 0),
                                      stop=(c == CC - 1)), reads=['ones_f', sqk], writes=[pe2k])
        mean, mk = m2t.next()
        P.op('act', 'activation', dict(out=mean[:, 0:tsz], in_=pm[:, 0:tsz], func=AF.Copy, scale=1.0 / C), reads=[pmk], writes=[mk])
        var, vk = m2t.next()
        P.op('dve', 'tensor_tensor', dict(out=var[:, 0:tsz], in0=mean[:, 0:tsz], in1=mean[:, 0:tsz], op=ALU.mult),
             reads=[mk], writes=[vk])
        P.op('dve', 'scalar_tensor_tensor', dict(out=var[:, 0:tsz], in0=pe2[:, 0:tsz], scalar=1.0 / C, in1=var[:, 0:tsz],
                                                 op0=ALU.mult, op1=ALU.subtract), reads=[pe2k, vk], writes=[vk])
        rstd_from_sumsq(cx, var[:, 0:tsz], vk, var[:, 0:tsz], vk, 1.0)
        un, unk = ucn.next()
        for c in range(CC):
            t1, t1k = st_f.next()
            eng = 'dve' if c % 2 == 0 else 'pool'
            P.op(eng, 'tensor_tensor', dict(out=t1[:, 0:tsz], in0=cv[:, c, t0:t0 + tsz], in1=mean[:, 0:tsz], op=ALU.subtract),
                 reads=cvk + [mk], writes=[t1k])
            P.op(eng, 'tensor_tensor', dict(out=t1[:, 0:tsz], in0=t1[:, 0:tsz], in1=var[:, 0:tsz], op=ALU.mult),
                 reads=[t1k, vk], writes=[t1k])
            P.op('act', 'activation', dict(out=un[:, c, 0:tsz], in_=t1[:, 0:tsz], func=AF.Silu, scale=cln_g[:, c:c + 1],
                                           bias=cln_b[:, c:c + 1]), reads=[t1k, 'vec'], writes=[unk])
        for n in range(CC):
            py, pyk = cx.psum(GP)
            mm_acc(cx, py[:, 0:tsz], pyk, lambda c: wpw[:, c, n * 128:(n + 1) * 128], lambda c: un[:, c, 0:tsz], CC,
                   ['wpw', unk])
            g_t, gk = gct.next()
            P.dma('sp', g_t[:, 0:tsz], gc_d[n * 128:(n + 1) * 128, HB + t0:HB + t0 + tsz], reads=['cg_d'], writes=[gk])
            s, sk = st_bf.next()
            P.op('dve', 'tensor_tensor', dict(out=s[:, 0:tsz], in0=py[:, 0:tsz], in1=g_t[:, 0:tsz], op=ALU.mult),
                 reads=[pyk, gk], writes=[sk])
            P.dma('sp', yc_d[n * 128:(n + 1) * 128, t0:t0 + tsz], s[:, 0:tsz], reads=[sk], writes=['yc_d'])

    cx.phase()
    hT = cx.sb([128, KC, TPF], BF16, "hTf")
    yfs = cx.sb([128, FC, TPF], BF16, "yfs")
    yms = cx.sb([128, MC, TPF], BF16, "yms")
    ycs = cx.sb([128, CC, TPF], BF16, "ycs")
    wrot = Rot(cx, 2, [128, KC, 512], BF16, "wsl2")
    wbr = Rot(cx, 2, [128, MC, 512], BF16, "wbr")
    gsb = Rot(cx, 2, [128, 512], F32, "gsb")
    acc = [cx.sb([128, TPF], F32, "acc%d" % i) for i in range(4)]
    brs = [(w_brf, F, yfs, 'yfs'), (w_brm, MLA, yms, 'yms'), (w_brc, C, ycs, 'ycs')]
    for (p0, tp) in splits(T, TPF):
        P.dma('sp', hT[:, :, 0:tp], hT_d[:, HB + p0:HB + p0 + tp].rearrange("(kc p) t -> p kc t", p=128),
              reads=['hT_d'], writes=['hTf'])
        P.dma('sp', yfs[:, :, 0:tp], yf_d[:, p0:p0 + tp].rearrange("(c p) t -> p c t", p=128), reads=['yf_d'], writes=['yfs'])
        P.dma('sp', yms[:, :, 0:tp], ym_d[:, p0:p0 + tp].rearrange("(c p) t -> p c t", p=128), reads=['ym_d'], writes=['yms'])
        P.dma('sp', ycs[:, :, 0:tp], yc_d[:, p0:p0 + tp].rearrange("(c p) t -> p c t", p=128), reads=['yc_d'], writes=['ycs'])
        items = [(d0, dn, bi) for (d0, dn) in splits(D, 512) for bi in range(3)]
        slots = {}

        def issue(ii):
            d0, dn, bi = items[ii]
            wb, kb_dim, ysb, yk = brs[bi]
            ws, wsk = wrot.next()
            load_w(cx, ws, wsk, w_in, 0, D, cfg['OFF_MERGE'] + bi * D + d0, dn)
            wbs, wbk = wbr.next()
            load_w(cx, wbs, wbk, wb, 0, kb_dim, d0, dn)
            slots[ii] = (ws, wsk, wbs, wbk)
        issue(0)
        for ii, (d0, dn, bi) in enumerate(items):
            if ii + 1 < len(items):
                issue(ii + 1)
            wb, kb_dim, ysb, yk = brs[bi]
            ws, wsk, wbs, wbk = slots[ii]
            for ds_ in range(dn // 128):
                pg, pgk = cx.psum(GP)
                mm_acc(cx, pg[:, 0:tp], pgk, lambda kc: ws[:, kc, ds_ * 128:(ds_ + 1) * 128], lambda kc: hT[:, kc, 0:tp],
                       KC, [wsk, 'hTf'])
                g_t, gk = gsb.next()
                P.op('act', 'activation', dict(out=g_t[:, 0:tp], in_=pg[:, 0:tp], func=AF.Sigmoid), reads=[pgk], writes=[gk])
                pb, pbk = cx.psum(GP)
                mm_acc(cx, pb[:, 0:tp], pbk, lambda kc: wbs[:, kc, ds_ * 128:(ds_ + 1) * 128],
                       lambda kc: ysb[:, kc, 0:tp], kb_dim // 128, [wbk, yk])
                ak = 'acc%d' % ds_
                if bi == 0:
                    P.op('dve', 'tensor_tensor', dict(out=acc[ds_][:, 0:tp], in0=pb[:, 0:tp], in1=g_t[:, 0:tp], op=ALU.mult),
                         reads=[pbk, gk], writes=[ak])
                else:
                    P.op('dve', 'tensor_tensor', dict(out=g_t[:, 0:tp], in0=pb[:, 0:tp], in1=g_t[:, 0:tp], op=ALU.mult),
                         reads=[pbk, gk], writes=[gk])
                    if bi == 1:
                        P.op('dve', 'tensor_tensor', dict(out=acc[ds_][:, 0:tp], in0=acc[ds_][:, 0:tp], in1=g_t[:, 0:tp],
                                                          op=ALU.add), reads=[ak, gk], writes=[ak])
                    else:
                        s, sk = st_bf.next()
                        P.op('dve', 'tensor_tensor', dict(out=s[:, 0:tp], in0=acc[ds_][:, 0:tp], in1=g_t[:, 0:tp],
                                                          op=ALU.add), reads=[ak, gk], writes=[sk])
                        dd = d0 + ds_ * 128
                        P.dma('sp', mg_d[dd:dd + 128, p0:p0 + tp], s[:, 0:tp], reads=[sk], writes=['mg_d'])

    cx.phase()
    mgs = cx.sb([128, KC, TPF], BF16, "mgs")
    wo = Rot(cx, 2, [128, KC, 512], BF16, "wo")
    gbc = cx.sb([128, D], F32, "gbc")
    P.dma('sp', gbc[:, :], gate_d[0:1, :].broadcast_to([128, D]), writes=['gbc'])
    NT = TPF // 128
    xts = [cx.sb([128, D], F32, "xo%d" % i) for i in range(NT)]
    ost = Rot(cx, 3, [128, 512], F32, "ost")
    for (p0, tp) in splits(T, TPF):
        P.dma('sp', mgs[:, :, 0:tp], mg_d[:, p0:p0 + tp].rearrange("(kc p) t -> p kc t", p=128), reads=['mg_d'], writes=['mgs'])
        tls = splits(tp, 128)
        for i, (t0, tsz) in enumerate(tls):
            P.dma('sp', xts[i][0:tsz, :], x_d[HB + p0 + t0:HB + p0 + t0 + tsz, :], writes=['xo%d' % i])
        for (n0, nn) in splits(D, 512):
            w_s, wk = wo.next()
            load_w(cx, w_s, wk, w_out, 0, D, n0, nn)
            for i, (t0, tsz) in enumerate(tls):
                po_, pok_ = cx.psum(GP)
                mm_acc(cx, po_[0:tsz, 0:nn], pok_, lambda kc: mgs[:, kc, t0:t0 + tsz], lambda kc: w_s[:, kc, 0:nn], KC,
                       [wk, 'mgs'])
                o_t, ok = ost.next()
                P.op('dve', 'tensor_tensor', dict(out=o_t[0:tsz, 0:nn], in0=po_[0:tsz, 0:nn], in1=gbc[0:tsz, n0:n0 + nn],
                                                  op=ALU.mult), reads=[pok_, 'gbc'], writes=[ok])
                P.op('pool', 'tensor_tensor', dict(out=o_t[0:tsz, 0:nn], in0=o_t[0:tsz, 0:nn], in1=xts[i][0:tsz, n0:n0 + nn],
                                                   op=ALU.add), reads=[ok, 'xo%d' % i], writes=[ok])
                P.dma('sp', out_d[p0 + t0:p0 + t0 + tsz, n0:n0 + nn], o_t[0:tsz, 0:nn], reads=[ok], writes=['out'])


def build_mod(D, NCOL, L):
    KC = D // 128
    nc = bass.Bass("TRN2", target_bir_lowering=False)
    cT = nc.dram_tensor("cT", [128, KC * 3], F32, kind="ExternalInput").ap()
    wa = nc.dram_tensor("wa", [L * D, NCOL], F32, kind="ExternalInput").ap()
    ba = nc.dram_tensor("ba", [L, NCOL], F32, kind="ExternalInput").ap()
    mod = nc.dram_tensor("mod", [L * 3, NCOL], F32, kind="ExternalOutput").ap()
    P = Prog(nc)
    sc = nc.alloc_sbuf_tensor("sc", [128, KC * 3], F32)
    wt = [nc.alloc_sbuf_tensor("wt%d" % i, [128, NCOL], F32) for i in range(3)]
    bt = nc.alloc_sbuf_tensor("bt", [3, NCOL], F32)
    ot = nc.alloc_sbuf_tensor("ot", [3, NCOL], F32)
    ch = splits(NCOL, 512)
    ps = [nc.alloc_psum_tensor("ps%d" % j, [128, 512], F32) for j in range(len(ch))]
    P.dma('sp', sc[:, :], cT[:, :], writes=['sc'])
    P.op('act', 'activation', dict(out=sc[:, :], in_=sc[:, :], func=AF.Silu), reads=['sc'], writes=['sc'])
    n = 0
    for l in range(L):
        P.dma('sp', bt[:, :], ba[l:l + 1, :].broadcast_to([3, NCOL]), writes=['bt'])
        for kc in range(KC):
            s = n % 3
            n += 1
            P.dma('sp', wt[s][:, :], wa[l * D + kc * 128: l * D + (kc + 1) * 128, :], writes=['wt%d' % s])
            for j, (c0, cn) in enumerate(ch):
                P.op('pe', 'matmul', dict(out=ps[j][0:3, 0:cn], lhsT=sc[:, kc * 3:(kc + 1) * 3], rhs=wt[s][:, c0:c0 + cn],
                                          start=(kc == 0), stop=(kc == KC - 1)), reads=['sc', 'wt%d' % s], writes=['ps%d' % j])
        for j, (c0, cn) in enumerate(ch):
            P.op('dve', 'tensor_tensor', dict(out=ot[:, c0:c0 + cn], in0=ps[j][0:3, 0:cn], in1=bt[:, c0:c0 + cn], op=ALU.add),
                 reads=['ps%d' % j, 'bt'], writes=['ot'])
        P.dma('sp', mod[l * 3:(l + 1) * 3, :], ot[:, :], reads=['ot'], writes=['mod'])
    P.finish_wait('sp', ['mod'])
    P.emit()
    return nc


BF = ml_dtypes.bfloat16
ROPE_BASE = 10000.0

def lay(v):
    return np.ascontiguousarray(np.asarray(v, np.float32).reshape(-1, 128).T)

def rope_tab(rows, cols, n, ROPE=64):
    out = np.zeros((ROPE, 2, n), np.float32)
    out[:, 0, :] = 1.0
    if rows is not None:
        hgrp = ROPE // 2
        half = hgrp // 2
        inv = (ROPE_BASE ** (-np.arange(half, dtype=np.float32) / half)).astype(np.float32)
        for grp, pos in enumerate((rows, cols)):
            ang = pos.astype(np.float32)[None, :] * inv[:, None]
            c = np.cos(ang).astype(np.float32); s = np.sin(ang).astype(np.float32)
            for hh in range(2):
                out[grp * hgrp + hh * half: grp * hgrp + (hh + 1) * half, 0, :] = c
                out[grp * hgrp + hh * half: grp * hgrp + (hh + 1) * half, 1, :] = s
    return out.reshape(ROPE, 2 * n)

def rot_mT(ROPE=64):
    Rm = np.zeros((ROPE, ROPE), np.float32)
    hgrp = ROPE // 2; half = hgrp // 2
    for p in range(ROPE):
        if (p % hgrp) < half:
            Rm[p, p + half] = -1.0
        else:
            Rm[p, p - half] = 1.0
    return np.ascontiguousarray(Rm.T)

def dft_seq(NS, k0, T):
    s = np.arange(NS, dtype=np.int64)[:, None]; k = (k0 + np.arange(T, dtype=np.int64))[None, :]
    ang = 2.0 * np.pi * ((s * k) % NS).astype(np.float64) / NS
    return np.cos(ang).astype(BF), np.sin(ang).astype(BF)

def dft_ch(F, NS):
    c = np.arange(F, dtype=np.int64)
    ang = 2.0 * np.pi * ((c[:, None] * c[None, :]) % F).astype(np.float64) / F
    sc = 1.0 / np.sqrt(float(NS) * F)
    return (np.cos(ang) * sc).astype(BF), (-np.sin(ang) * sc).astype(BF)

def make_vecs(cfg, q_a_norm, q_norm, k_norm, conv_b, cln_g, cln_b, conv_w):
    NOPE, ROPE, C, CW = cfg['NOPE'], cfg['ROPE'], cfg['C'], cfg['CW']
    CC = C // 128
    pad = lambda v: np.concatenate([np.asarray(v, np.float32), np.zeros(128 - len(v), np.float32)])[:, None]
    cw = np.asarray(conv_w, np.float32).reshape(CW, CC, 128).transpose(2, 1, 0).reshape(128, CC * CW)
    return np.ascontiguousarray(np.concatenate([
        lay(q_a_norm), np.asarray(q_norm[:NOPE], np.float32)[:, None], pad(q_norm[NOPE:]),
        np.asarray(k_norm[:NOPE], np.float32)[:, None], pad(k_norm[NOPE:]),
        lay(conv_b), lay(cln_g), lay(cln_b), cw], axis=1))


from concourse.bass_utils import run_bass_kernel_spmd

FULL = dict(D=4096, F=1024, QL=768, KVL=512, NH=16, C=1024, B=2, S=8192, NCTX=256, L=2, GW=64)
NCORES = 8


def _run(nc, in_maps):
    res = run_bass_kernel_spmd(nc, in_maps, core_ids=list(range(NCORES)))
    return res.results


def run_model(inp, dm, TPB=768, TPF=512, TPP=768):
    D, F, QL, KVL, NH, C, B, S, NCTX, L, GW = [dm[k] for k in ('D', 'F', 'QL', 'KVL', 'NH', 'C', 'B', 'S', 'NCTX', 'L', 'GW')]
    f32 = lambda a: np.ascontiguousarray(np.asarray(a, dtype=np.float32))
    inp = {k: f32(v) for k, v in inp.items()}
    QPB = NCORES // B
    T = S // QPB
    CS = NCTX // QPB
    HB = 128
    cfgL = mkcfg(D=D, F=F, QL=QL, KVL=KVL, NH=NH, C=C, T=T, HB=HB, NK=S + NCTX, NS=S)
    cfgC = mkcfg(D=D, F=F, QL=QL, KVL=KVL, NH=NH, C=C, T=NCTX, HB=HB, NK=NCTX, NS=NCTX)
    KC = D // 128
    ROPE = cfgL['ROPE']
    ident = np.eye(128, dtype=np.float32)
    c3 = np.concatenate([inp['c'], inp['c_ctx'][None, :]], 0)
    assert B == 2
    cT = np.ascontiguousarray(c3.T.reshape(KC, 128, 3).transpose(1, 0, 2)).reshape(128, KC * 3)
    NCOL = 3 * D // NCORES
    nc_mod = build_mod(D, NCOL, L)
    ims = []
    for i in range(NCORES):
        ims.append(dict(cT=cT,
                        wa=np.ascontiguousarray(inp['w_ada'][:, :, i * NCOL:(i + 1) * NCOL]).reshape(L * D, NCOL),
                        ba=np.ascontiguousarray(inp['b_ada'][:, i * NCOL:(i + 1) * NCOL])))
    r = _run(nc_mod, ims)
    mod = np.concatenate([x["mod"].reshape(L, 3, NCOL) for x in r], axis=2)
    del ims, r
    rows = np.repeat(np.arange(S // GW, dtype=np.int32), GW)
    cols = np.tile(np.arange(GW, dtype=np.int32), S // GW)
    rk_ctx = rope_tab(None, None, NCTX, ROPE).reshape(ROPE, 2, NCTX)
    rk_lat = rope_tab(rows, cols, S, ROPE).reshape(ROPE, 2, S)
    ropek_L = np.ascontiguousarray(np.concatenate([rk_ctx, rk_lat], 2)).reshape(ROPE, 2 * (S + NCTX))
    ropek_C = np.ascontiguousarray(rk_ctx).reshape(ROPE, 2 * NCTX)
    rmt = rot_mT(ROPE)
    ccL, nscL = dft_ch(F, S)
    ccC, nscC = dft_ch(F, NCTX)
    cstC, sstC = dft_seq(NCTX, 0, NCTX)
    seqtab = [dft_seq(S, j * T, T) for j in range(QPB)]
    ropeq_L = [rope_tab(rows[j * T:(j + 1) * T], cols[j * T:(j + 1) * T], T, ROPE) for j in range(QPB)]
    ropeq_C = rope_tab(None, None, NCTX, ROPE)
    pre_tiles = [(a, b, 0) for (a, b) in splits(T, 128)] + [(T + a, b, 1) for (a, b) in splits(CS, 128)]
    nc_pre = build_pre(cfgL, pre_tiles, TP=TPP)
    xl = inp['x']
    xc = inp['ctx']
    for l in range(L):
        last = (l == L - 1)
        shift_l, scale_l, gate_l = mod[l, :B, 0:D], mod[l, :B, D:2 * D], mod[l, :B, 2 * D:3 * D]
        shift_c, scale_c, gate_c = mod[l, B, 0:D], mod[l, B, D:2 * D], mod[l, B, 2 * D:3 * D]
        w_in = inp['w_in'][l]
        wu = np.ascontiguousarray(w_in[:, 0:F])
        wkv = np.ascontiguousarray(w_in[:, cfgL['OFF_KV']:cfgL['OFF_MG']])
        kvg = lay(inp['kv_a_norm'][l])
        ims = []
        for i in range(NCORES):
            b, j = i // QPB, i % QPB
            xs = np.concatenate([xl[b, j * T:(j + 1) * T], xc[b, j * CS:(j + 1) * CS]], 0)
            mods = np.concatenate([lay(inp['norm_g'][l]), lay(scale_l[b]), lay(shift_l[b]), lay(scale_c), lay(shift_c)], 1)
            ims.append(dict(x=np.ascontiguousarray(xs), ident=ident, mods=np.ascontiguousarray(mods), wu=wu, wkv=wkv, kvg=kvg))
        r = _run(nc_pre, ims)
        del ims
        u_lat, u_ctx, ckvn_b, kr_b = [], [], [], []
        for b in range(B):
            rs = r[b * QPB:(b + 1) * QPB]
            u_lat.append(np.ascontiguousarray(np.concatenate([x["u_o"][0:T] for x in rs], 0)))
            u_ctx.append(np.ascontiguousarray(np.concatenate([x["u_o"][T:T + CS] for x in rs], 0)))
            ckvn_b.append(np.ascontiguousarray(np.concatenate([x["ckv_o"][:, T:T + CS] for x in rs] + [x["ckv_o"][:, 0:T] for x in rs], 1)))
            kr_b.append(np.ascontiguousarray(np.concatenate([x["kr_o"][:, T:T + CS] for x in rs] + [x["kr_o"][:, 0:T] for x in rs], 1)))
        del r
        blocks = [(cfgL, "")] if last else [(cfgL, ""), (cfgC, "c_")]
        nc_blk = build_block(blocks, TPB=TPB, TPF=TPF)
        vecs = make_vecs(cfgL, inp['q_a_norm'][l], inp['q_norm'][l], inp['k_norm'][l], inp['conv_b'][l], inp['cln_g'][l],
                         inp['cln_b'][l], inp['conv_w'][l])
        shared = dict(ident=ident, rmt=rmt, vecs=vecs, w_in=w_in, w_uq=inp['w_uq'][l], w_ukv=inp['w_ukv'][l],
                      w_fnet=inp['w_fnet'][l], w_pw2=inp['w_pw2'][l], w_br_f=inp['w_br_f'][l], w_br_m=inp['w_br_m'][l],
                      w_br_c=inp['w_br_c'][l], w_out=inp['w_out'][l])
        ims = []
        for i in range(NCORES):
            b, j = i // QPB, i % QPB
            TE = T + 2 * HB
            x_ext = np.zeros((TE, D), np.float32)
            hmask = np.zeros((1, TE), np.float32)
            lo = j * T - HB
            v0, v1 = max(lo, 0), min(lo + TE, S)
            x_ext[v0 - lo:v1 - lo] = xl[b, v0:v1]
            hmask[0, v0 - lo:v1 - lo] = 1.0
            m = dict(shared)
            m.update(dict(x=x_ext, mods=np.ascontiguousarray(np.concatenate([lay(inp['norm_g'][l]), lay(scale_l[b]), lay(shift_l[b])], 1)),
                          gate=np.ascontiguousarray(gate_l[b][None, :]), hmask=hmask, ckvn=ckvn_b[b], kr=kr_b[b], u_all=u_lat[b],
                          ropeq=ropeq_L[j], ropek=ropek_L, cst=seqtab[j][0], sst=seqtab[j][1], cc=ccL, nsc=nscL))
            if not last:
                TEc = NCTX + 2 * HB
                xc_ext = np.zeros((TEc, D), np.float32)
                cmask = np.zeros((1, TEc), np.float32)
                xc_ext[HB:HB + NCTX] = xc[b]
                cmask[0, HB:HB + NCTX] = 1.0
                m.update(dict(c_x=xc_ext, c_mods=np.ascontiguousarray(np.concatenate([lay(inp['norm_g'][l]), lay(scale_c), lay(shift_c)], 1)),
                              c_gate=np.ascontiguousarray(gate_c[None, :]), c_hmask=cmask,
                              c_ckvn=np.ascontiguousarray(ckvn_b[b][:, 0:NCTX]), c_kr=np.ascontiguousarray(kr_b[b][:, 0:NCTX]),
                              c_u_all=u_ctx[b], c_ropeq=ropeq_C, c_ropek=ropek_C, c_cst=cstC, c_sst=sstC, c_cc=ccC, c_nsc=nscC))
            ims.append(m)
        r = _run(nc_blk, ims)
        del ims
        xl = np.stack([np.concatenate([r[b * QPB + j]["out"] for j in range(QPB)], 0) for b in range(B)], 0)
        if not last:
            xc = np.stack([r[b * QPB]["c_out"] for b in range(B)], 0)
        del r
    return np.ascontiguousarray(xl.astype(np.float32))


def kernel(**inputs):
    return run_model(inputs, FULL)
```
